# Optimizing a Trainium2 kernel written in Bass

```python
import jax, jax.numpy as jnp
from jax import lax
import numpy as np

D_MODEL = 1024
BATCH = 8
SEQ = 2048
DEPTH = 1

CHUNK = 64
D_MIX = D_MODEL
D_HGRN = D_MIX // 2
D_CONV = D_MIX - D_HGRN
HGRN_HEAD_DIM = 128
HGRN_HEADS = D_HGRN // HGRN_HEAD_DIM
CONV_WIDTH = 3
CONV_GROUPS = 8
EPS = 1e-6
IN_COLS = 4 * D_HGRN + 4 * D_CONV

kernel_name = "hgrn2_shortconv_parallel_hybrid"


def rmsnorm(x, g):
    xf = x.astype(jnp.float32)
    y = xf * lax.rsqrt(jnp.mean(xf * xf, axis=-1, keepdims=True) + EPS)
    return (y * g.astype(jnp.float32)).astype(x.dtype)


def grouped_rmsnorm(x, g, n_groups):
    bsz, s, w = x.shape
    xf = x.astype(jnp.float32).reshape(bsz, s, n_groups, w // n_groups)
    y = xf * lax.rsqrt(jnp.mean(xf * xf, axis=-1, keepdims=True) + EPS)
    return (y.reshape(bsz, s, w) * g.astype(jnp.float32)).astype(x.dtype)


def hgrn2_chunkwise(q, log_f, k, v):
    bsz, s, h, dk = q.shape
    dv = v.shape[-1]
    nc = s // CHUNK

    def split(t):
        return t.reshape(bsz, nc, CHUNK, h, t.shape[-1]).transpose(0, 3, 1, 2, 4)

    q, log_f, k, v = (split(t).astype(jnp.float32) for t in (q, log_f, k, v))
    b = jnp.cumsum(log_f, axis=3)
    g = b[..., -1:, :]
    q_dec = q * jnp.exp(b)
    k_inv = k * jnp.exp(-b)
    k_end = k * jnp.exp(g - b)
    causal = jnp.tril(jnp.ones((CHUNK, CHUNK), dtype=bool))
    scores = jnp.einsum('bhnck,bhnsk->bhncs', q_dec, k_inv)
    scores = jnp.where(causal, scores, 0.0)
    o_intra = jnp.einsum('bhncs,bhnsv->bhncv', scores, v)
    chunk_update = jnp.einsum('bhnsk,bhnsv->bhnkv', k_end, v)
    chunk_decay = jnp.exp(g[..., 0, :])

    def step(state, inp):
        dec, upd = inp
        return dec[..., None] * state + upd, state

    s0 = jnp.zeros((bsz, h, dk, dv), jnp.float32)
    _, s_before = lax.scan(step, s0, (jnp.moveaxis(chunk_decay, 2, 0),
                                      jnp.moveaxis(chunk_update, 2, 0)))
    s_before = jnp.moveaxis(s_before, 0, 2)
    o_inter = jnp.einsum('bhnck,bhnkv->bhncv', q_dec, s_before)
    o = o_intra + o_inter
    return o.transpose(0, 2, 3, 1, 4).reshape(bsz, s, h, dv)


def causal_depthwise_conv(u, w):
    c = u.shape[-1]
    return lax.conv_general_dilated(
        u, w.astype(u.dtype)[:, None, :], window_strides=(1,),
        padding=[(CONV_WIDTH - 1, 0)],
        dimension_numbers=('NWC', 'WIO', 'NWC'), feature_group_count=c)


def setup_inputs(seed: int = 0) -> dict:
    key = jax.random.key(seed)
    ks = jax.random.split(key, 9)
    f32 = jnp.float32
    x = jax.random.normal(ks[0], (BATCH, SEQ, D_MODEL), f32)
    norm_gain = 1.0 + 0.02 * jax.random.normal(ks[1], (DEPTH, D_MODEL), f32)
    w_in = jax.random.normal(ks[2], (DEPTH, D_MODEL, IN_COLS), f32) * D_MODEL ** -0.5
    lb_logits = 1.0 + 0.1 * jax.random.normal(ks[3], (DEPTH + 1, D_HGRN), f32)
    conv_w = jax.random.normal(ks[4], (DEPTH, CONV_WIDTH, D_CONV), f32) * CONV_WIDTH ** -0.5
    hgrn_norm_gain = 1.0 + 0.02 * jax.random.normal(ks[5], (DEPTH, D_HGRN), f32)
    conv_norm_gain = 1.0 + 0.02 * jax.random.normal(ks[6], (DEPTH, D_CONV), f32)
    w_out = jax.random.normal(ks[7], (DEPTH, D_MIX, D_MODEL), f32) * D_MIX ** -0.5
    final_norm_gain = 1.0 + 0.02 * jax.random.normal(ks[8], (D_MODEL,), f32)
    return {"x": x, "norm_gain": norm_gain, "w_in": w_in, "lb_logits": lb_logits,
            "conv_w": conv_w, "hgrn_norm_gain": hgrn_norm_gain,
            "conv_norm_gain": conv_norm_gain, "w_out": w_out,
            "final_norm_gain": final_norm_gain}


def reference(x, norm_gain, w_in, lb_logits, conv_w, hgrn_norm_gain,
              conv_norm_gain, w_out, final_norm_gain):
    bsz, s, _ = x.shape
    lower_bounds = jnp.cumsum(jax.nn.softmax(lb_logits.astype(jnp.float32), axis=0), axis=0)
    splits = [D_HGRN, 2 * D_HGRN, 3 * D_HGRN, 4 * D_HGRN,
              4 * D_HGRN + D_CONV, 4 * D_HGRN + 2 * D_CONV, 4 * D_HGRN + 3 * D_CONV]
    for l in range(DEPTH):
        h = rmsnorm(x, norm_gain[l])
        proj = jnp.einsum('bsd,dc->bsc', h, w_in[l])
        q, f_logit, i_in, z_a, u, gate_b, gate_c, z_b = jnp.split(proj, splits, axis=-1)

        lb = lower_bounds[l]
        f = lb + (1.0 - lb) * jax.nn.sigmoid(f_logit.astype(jnp.float32))
        log_f = jnp.log(f)
        k = 1.0 - f
        hs = (bsz, s, HGRN_HEADS, HGRN_HEAD_DIM)
        o_a = hgrn2_chunkwise(q.reshape(hs), log_f.reshape(hs), k.reshape(hs), i_in.reshape(hs))
        o_a = grouped_rmsnorm(o_a.reshape(bsz, s, D_HGRN).astype(x.dtype),
                              hgrn_norm_gain[l], HGRN_HEADS)
        o_a = o_a * jax.nn.silu(z_a)

        y_b = gate_b * causal_depthwise_conv(gate_c * u, conv_w[l])
        o_b = grouped_rmsnorm(y_b, conv_norm_gain[l], CONV_GROUPS) * jax.nn.silu(z_b)

        mixed = jnp.concatenate([o_a, o_b], axis=-1)
        x = x + jnp.einsum('bsc,cd->bsd', mixed, w_out[l])
    return rmsnorm(x, final_norm_gain)
```

```python
import contextlib
import numpy as np
import concourse.bass as bass
import concourse.mybir as mybir
from concourse.bass_utils import run_bass_kernel_spmd

F32 = mybir.dt.float32
BF16 = mybir.dt.bfloat16
AF = mybir.ActivationFunctionType
ALU = mybir.AluOpType

S = 2048
D = 1024
NCOL = 4096
TB = 512
NB = S // TB
NT = S // 128
EPS = 1e-6
GQ, GF, GV, GZA, GU, GGB, GGC, GZB = range(8)
NPS = 6
NTR = 2
NXT = 4
NDMASEM = {"sp": 16, "pool": 16, "act": 4}
HOP = 600.0
WINDOW = 420


class VB:
    __slots__ = ("pool", "p", "ops", "idx")

    def __init__(self, pool):
        self.pool = pool
        self.p = None
        self.ops = []
        self.idx = -1


class _Op:
    pass


class Prog:
    ENGS = ("pe", "act", "dve", "pool", "sp")

    def __init__(self):
        self.ops = {e: [] for e in self.ENGS}
        self.all = []
        self.lastw = {}
        self.readers = {}
        self.dmas = {e: [] for e in self.ENGS}
        self.vbs = {}

    def vb(self, pool="ps"):
        v = VB(pool)
        lst = self.vbs.setdefault(pool, [])
        v.idx = len(lst)
        lst.append(v)
        return v

    def add(self, eng, fn, reads=(), writes=(), cost=500.0, dma=False):
        op = _Op()
        op.eng, op.fn, op.dma, op.cost = eng, fn, dma, float(cost)
        op.gidx = len(self.all)
        op.sig = False
        op.sigval = 0
        op.waits = []
        op.acq = None
        op.vbl = []
        op.tset = None
        deps = []
        for k in reads:
            w = self.lastw.get(k)
            if w is not None:
                deps.append((w, "raw"))
            if isinstance(k, VB):
                for r in self.readers.get(k, ()):
                    if r.eng != eng:
                        deps.append((r, "war"))
        for k in writes:
            w = self.lastw.get(k)
            if w is not None:
                deps.append((w, "waw"))
            for r in self.readers.get(k, ()):
                deps.append((r, "war"))
        for k in list(reads) + list(writes):
            if isinstance(k, VB):
                if not k.ops:
                    assert k in writes, "first access of a psum bank must be a write"
                    op.acq = k
                if not k.ops or k.ops[-1] is not op:
                    k.ops.append(op)
                if k not in op.vbl:
                    op.vbl.append(k)
        for k in reads:
            self.readers.setdefault(k, []).append(op)
        for k in writes:
            self.lastw[k] = op
            self.readers[k] = []
        op.deps = deps
        if dma:
            lst = self.dmas[eng]
            nsem = NDMASEM[eng]
            op.dsem = (eng, len(lst) % nsem)
            op.dval = 16 * (len(lst) // nsem + 1)
            if len(lst) >= nsem:
                deps.append((lst[len(lst) - nsem], "raw"))
            lst.append(op)
        self.all.append(op)
        return op

    def schedule(self, pools):
        ops = self.all
        for op in ops:
            op.succ = []
            op.done = False
            op.start = op.end = 0.0
        for op in ops:
            preds = {id(p): p for (p, _) in op.deps if p is not op}
            op.npred = len(preds)
            for p in preds.values():
                p.succ.append(op)
        for op in reversed(ops):
            op.blevel = op.cost + max((s_.blevel for s_ in op.succ), default=0.0)
        free = {e: 0.0 for e in self.ENGS}
        bank_free = {pl: {b: 0.0 for b in ids} for pl, ids in pools.items()}
        bank_prev = {pl: {b: None for b in ids} for pl, ids in pools.items()}
        ready = [op for op in ops if op.npred == 0]
        pool_dmas = [op for op in ops if op.dma and op.eng == "pool"]
        pool_ptr = [0]
        cur_tset = [None]
        dma_pipe = [0.0]
        nsched = 0
        lowest = 0
        order = {e: [] for e in self.ENGS}
        n = len(ops)
        while nsched < n:
            while lowest < n and ops[lowest].done:
                lowest += 1
            best = None
            for op in ready:
                if op.gidx > lowest + WINDOW:
                    continue
                if op.dma and op.eng == "pool" and pool_dmas[pool_ptr[0]] is not op:
                    continue
                st = free[op.eng]
                for (p, kind) in op.deps:
                    if p is op:
                        continue
                    if p.eng == op.eng and not (op.dma or p.dma):
                        t = p.end if op.eng == "pe" else p.end + 60.0
                    else:
                        t = p.end + HOP
                    if t > st:
                        st = t
                if op.tset is not None and op.tset != cur_tset[0]:
                    st += 2400.0
                bank = None
                if op.acq is not None:
                    vb = op.acq
                    pl = vb.pool
                    cands = []
                    for b in pools[pl]:
                        pv = bank_prev[pl][b]
                        if pv is None or all(o.done for o in pv.ops):
                            cands.append(b)
                    if not cands:
                        continue
                    earlier_unacq = sum(1 for v in self.vbs[pl][:vb.idx] if v.p is None)
                    if earlier_unacq > 0 and len(cands) <= min(2, earlier_unacq):
                        continue
                    bank = min(cands, key=lambda b: bank_free[pl][b])
                    st = max(st, bank_free[pl][bank] + HOP)
                key = (st, 0.0 if op.dma else -op.blevel, op.gidx)
                if best is None or key < best[0]:
                    best = (key, op, bank)
            if best is None:
                raise RuntimeError("scheduler stuck at op %d" % lowest)
            (st, _, _), op, bank = best
            if op.acq is not None:
                vb = op.acq
                pl = vb.pool
                vb.p = bank
                pv = bank_prev[pl][bank]
                if pv is not None:
                    for o in pv.ops:
                        op.deps.append((o, "war"))
                bank_prev[pl][bank] = vb
            if op.tset is not None:
                cur_tset[0] = op.tset
            op.start = st
            op.end = st + op.cost
            if op.dma and getattr(op, "nbytes", 0):
                d0 = max(st + 1500.0, dma_pipe[0])
                dma_pipe[0] = d0 + op.nbytes / 330.0
                op.end = dma_pipe[0] + 500.0
            free[op.eng] = (st + (1100.0 if op.eng == "pool" else 80.0)) if op.dma else op.end
            for k in op.vbl:
                if op.end > bank_free[k.pool][k.p]:
                    bank_free[k.pool][k.p] = op.end
            op.done = True
            if op.dma and op.eng == "pool":
                pool_ptr[0] += 1
            nsched += 1
            ready.remove(op)
            order[op.eng].append(op)
            for s_ in op.succ:
                s_.npred -= 1
                if s_.npred == 0:
                    ready.append(s_)
        for e in self.ENGS:
            self.ops[e] = order[e]
            for i, op in enumerate(order[e]):
                op.idx = i
        self.makespan = max(op.end for op in ops)

    def finalize(self):
        for e in self.ENGS:
            waited = {f: -1 for f in self.ENGS}
            dma_waited = set()
            for op in self.ops[e]:
                need = {}
                dneed = []
                for (p, kind) in op.deps:
                    if p is op:
                        continue
                    if p.dma:
                        if id(p) not in dma_waited:
                            dma_waited.add(id(p))
                            dneed.append(p)
                        continue
                    if p.eng == e and not op.dma:
                        if e == "pe":
                            continue
                    if p.idx <= waited[p.eng]:
                        continue
                    if p.idx > need.get(p.eng, -1):
                        need[p.eng] = p.idx
                for f, idx in need.items():
                    waited[f] = idx
                    prod = self.ops[f][idx]
                    prod.sig = True
                    op.waits.append(("c", prod))
                for p in dneed:
                    op.waits.append(("d", p))
        for e in self.ENGS:
            c = 0
            for op in self.ops[e]:
                if op.sig:
                    c += 1
                op.sigval = c

    def emit(self, nc, block, sems, dsems):
        handles = {"pe": block.tensor, "act": block.scalar, "dve": block.vector,
                   "pool": block.gpsimd, "sp": block.sync}
        for e in self.ENGS:
            ops = self.ops[e]
            if not ops:
                continue

            def body(eng, ops=ops, e=e):
                for op in ops:
                    for (kind, p) in op.waits:
                        if kind == "c":
                            eng.wait_ge(sems[p.eng], p.sigval)
                        else:
                            eng.wait_ge(dsems[p.dsem], p.dval)
                    ins = op.fn(eng)
                    if op.dma:
                        ins.then_inc(dsems[op.dsem], 16)
                    elif op.sig:
                        ins.then_inc(sems[e], 1)

            handles[e](body)


def _mmc(n, fp32=False):
    return (max(n, 64) * (4 if fp32 else 1)) / 2.1 + 20.0


def _actc(n):
    return 240.0 + n / 1.2


def _dvec(n, psum=False, fast=False):
    return (160.0 if psum else 100.0) + n / (1.9 if fast else 0.96)


def build_nc():
    nc = bass.Bass("TRN2", target_bir_lowering=False)
    try:
        nc.allow_low_precision("bf16 matmul operands, fp32 accumulation")
    except Exception:
        pass
    x_d = nc.dram_tensor("x", [S, D], F32, kind="ExternalInput").ap()
    win_d = nc.dram_tensor("w_in", [D, NCOL], F32, kind="ExternalInput").ap()
    wout_d = nc.dram_tensor("w_out", [D, D], F32, kind="ExternalInput").ap()
    pp_d = nc.dram_tensor("pp", [128, 32], F32, kind="ExternalInput").ap()
    cb_d = nc.dram_tensor("cb", [128, 1280], F32, kind="ExternalInput").ap()
    ng_d = nc.dram_tensor("ng", [1, D], F32, kind="ExternalInput").ap()
    fg_d = nc.dram_tensor("fg", [1, D], F32, kind="ExternalInput").ap()
    out_d = nc.dram_tensor("out", [S, D], F32, kind="ExternalOutput").ap()

    P = Prog()
    es = contextlib.ExitStack()
    with es:
        def sb(name, shape, dt):
            return es.enter_context(nc.sbuf_tensor(name, shape, dt))

        def psb(name, shape, dt):
            return es.enter_context(nc.psum_tensor(name, shape, dt))

        win = sb("win", [128, 8, NCOL], BF16)
        wout = sb("wout", [128, 8, D], BF16)
        pp = sb("pp_sb", [128, 32], F32)
        pd = sb("pd_sb", [128, 32], F32)
        cb = sb("cb_sb", [128, 1280], BF16)
        gbc = sb("gbc", [128, D], F32)
        fgbc = sb("fgbc", [128, D], F32)
        xt = [sb(f"xt{i}", [128, D], F32) for i in range(NXT)]
        xn = [sb(f"xn{i}", [128, D], BF16) for i in range(2)]
        ss = sb("ss", [128, NT], F32)
        epsb = sb("epsb", [128, 1], F32)
        rstd = sb("rstd", [128, NT], F32)
        ss2 = sb("ss2", [128, NT], F32)
        rstd2 = sb("rstd2", [128, NT], F32)
        hT = [sb(f"hT{i}", [128, 8, TB], BF16) for i in range(2)]
        tt = [sb(f"tt{h}", [128, TB], F32) for h in range(4)]
        bb = [sb(f"bb{h}", [128, TB], F32) for h in range(4)]
        LL = [sb(f"LL{i}", [128, TB], F32) for i in range(2)]
        decb = sb("decb", [128, 4, S // 64], F32)
        qdec = sb("qdec", [128, 4, TB], BF16)
        kinvT = sb("kinvT", [128, 4, TB], BF16)
        ktok = sb("ktok", [128, 4, 512], BF16)
        vbf = sb("vbf", [128, 4, 512], BF16)
        scm = sb("scm", [128, 4, 512], BF16)
        Sbf = sb("Sbf", [128, 7, 256], BF16)
        Sb0 = sb("Sb0", [128, 2, 512], BF16)
        Tst = sb("Tst", [128, 512], F32)
        sa = [sb(f"sa{i}", [128, TB], F32) for i in range(2)]
        osq = [sb(f"osq{i}", [128, TB], BF16) for i in range(2)]
        rs = [sb(f"rs{i}", [128, TB], F32) for i in range(2)]
        usb = [sb(f"usb{i}", [128, TB], F32) for i in range(1)]
        cu = [sb(f"cu{c}", [128, TB + 2], F32) for c in range(4)]
        ct = [sb(f"ct{i}", [128, TB], F32) for i in range(2)]
        rc = [sb(f"rc{i}", [128, 8], F32) for i in range(2)]
        rh = [sb(f"rh{i}", [128, 8], BF16) for i in range(2)]
        rl = [sb(f"rl{i}", [128, 8], BF16) for i in range(2)]
        mixT = sb("mixT", [128, 8, TB], BF16)
        ps = [psb(f"ps{i}", [128, 512], F32) for i in range(NPS)]
        trs = [psb(f"tr{i}", [128, 1024], BF16) for i in range(NTR)]

        sems = {e: es.enter_context(nc.semaphore(f"s_{e}")) for e in Prog.ENGS}
        dsems = {(e, i): es.enter_context(nc.semaphore(f"d_{e}{i}"))
                 for e in NDMASEM for i in range(NDMASEM[e])}

        ident = cb[:, 0:128]
        ones = cb[:, 128:256]
        ind2 = cb[:, 256:258]
        L1b = cb[:, 1024:1152]
        L0b = cb[:, 1152:1280]
        mask1 = cb[:, 384:512]
        scanmask = cb[:, 512:1024]

        def PS(v):
            return ps[v.p]

        def TR(v):
            return trs[v.p]

        add = P.add

        def dma(eng, out, in_, reads, writes, nbytes):
            op = P.add(eng, lambda e: e.dma_start(out=out, in_=in_), reads, writes,
                       cost=2200.0 + nbytes / 180.0, dma=True)
            op.nbytes = nbytes
            return op

        xpre = {}
        for ti in range(4):
            xpre[ti] = ti % NXT
            dma("sp", xt[ti % NXT][:], x_d[ti * 128:(ti + 1) * 128, :], [], [("xt", ti % NXT)],
                524288)
            if ti == 0:
                dma("sp", gbc[:], ng_d.partition_broadcast(128), [], ["gbc"], 524288)
        dma("pool", cb[:, 0:128], cb_d[:, 0:128], [], ["cbi"], 65536)
        dma("sp", pp[:], pp_d, [], ["pp"], 16384)
        dma("sp", fgbc[:], fg_d.partition_broadcast(128), [], ["fgbc"], 524288)

        xslot = [0]

        def alloc_xslot():
            s_ = xslot[0] % NXT
            xslot[0] += 1
            return s_

        def load_x(ti, slot):
            dma("sp", xt[slot][:], x_d[ti * 128:(ti + 1) * 128, :], [("win", GZB)], [("xt", slot)],
                524288)

        win_v = win_d.rearrange("(k p) c -> p k c", p=128)
        wout_v = wout_d.rearrange("(k p) c -> p k c", p=128)
        for sub in range(4):
            c0 = GF * 512 + sub * 128
            dma("pool", win[:, :, c0:c0 + 128], win_v[:, :, c0:c0 + 128],
                [("xt", 3)] if sub == 0 else [], [("win", GF, sub)], 524288)
            if sub == 0:
                dma("pool", cb[:, 128:1280], cb_d[:, 128:1280], [], ["cb"], 589824)
        for g in (GV, GQ, GU, GGC, GGB, GZB, GZA):
            dma("pool", win[:, :, g * 512:(g + 1) * 512], win_v[:, :, g * 512:(g + 1) * 512],
                [], [("win", g)], 2097152)
        for hf in range(2):
            dma("pool", wout[:, :, hf * 512:(hf + 1) * 512], wout_v[:, :, hf * 512:(hf + 1) * 512],
                [], [("wout", hf)], 2097152)

        add("dve", lambda e: e.memset(ss[:], 0.0), [], ["ss"], 100)
        add("dve", lambda e: e.memset(epsb[:], float(EPS)), [], ["epsb"], 100)
        add("dve", lambda e: e.memset(ss2[:], 0.0), [], ["ss2"], 100)
        for c in range(4):
            add("dve", lambda e, c=c: e.memset(cu[c][:, 0:2], 0.0), [], [("cuh", c)], 100)

        add("dve", lambda e: e.tensor_tensor(out=pd[:, 0:4], in0=pp[:, 4:8], in1=pp[:, 0:4],
                                             op=ALU.subtract), ["pp"], ["pd0"], 150)
        add("act", lambda e: e.activation(out=pd[:, 0:4], in_=pd[:, 0:4], func=AF.Exp),
            ["pd0"], ["pd0"], 2000)
        add("dve", lambda e: e.memset(pd[:, 28:29], -0.5), [], ["mh"], 100)
        add("dve", lambda e: e.tensor_scalar(out=pd[:, 0:4], in0=pd[:, 0:4], scalar1=1.0,
                                             scalar2=None, op0=ALU.add), ["pd0"], ["pd0"], 150)
        add("dve", lambda e: e.reciprocal(out=pd[:, 4:8], in_=pd[:, 0:4]), ["pd0"], ["pd1"], 150)
        add("dve", lambda e: e.tensor_scalar(out=pd[:, 8:12], in0=pd[:, 4:8], scalar1=-0.5,
                                             scalar2=0.5, op0=ALU.mult, op1=ALU.add),
            ["pd1"], ["pd2"], 150)
        add("dve", lambda e: e.tensor_scalar(out=pd[:, 12:16], in0=pd[:, 8:12], scalar1=-1.0,
                                             scalar2=None, op0=ALU.mult), ["pd2"], ["pd3"], 150)
        add("dve", lambda e: e.tensor_scalar(out=pd[:, 16:20], in0=pd[:, 8:12], scalar1=-1.0,
                                             scalar2=1.0, op0=ALU.mult, op1=ALU.add),
            ["pd2"], ["pd4"], 150)
        add("dve", lambda e: e.tensor_scalar(out=pd[:, 20:24], in0=pp[:, 20:24],
                                             scalar1=float(np.sqrt(128.0)), scalar2=None,
                                             op0=ALU.mult), ["pp"], ["pd5"], 150)
        add("dve", lambda e: e.tensor_scalar(out=pd[:, 24:28], in0=pp[:, 24:28], scalar1=8.0,
                                             scalar2=None, op0=ALU.mult), ["pp"], ["pd6"], 150)
        def _mk(tag):
            def emit(e, vb, pairs):
                k = len(pairs)
                ins = None
                for i, (l, r) in enumerate(pairs):
                    ins = e.matmul(PS(vb)[:], l, r, start=(i == 0), stop=(i == k - 1))
                return ins
            return emit

        def _e_f(e, vb, pairs):
            ins = None
            for i, (l, r) in enumerate(pairs):
                ins = e.matmul(PS(vb)[:], l, r, start=(i == 0), stop=(i == len(pairs) - 1))
            return ins

        def _e_v(e, vb, pairs):
            ins = None
            for i, (l, r) in enumerate(pairs):
                ins = e.matmul(PS(vb)[:], l, r, start=(i == 0), stop=(i == len(pairs) - 1))
            return ins

        def _e_q(e, vb, pairs):
            ins = None
            for i, (l, r) in enumerate(pairs):
                ins = e.matmul(PS(vb)[:], l, r, start=(i == 0), stop=(i == len(pairs) - 1))
            return ins

        def _e_c(e, vb, pairs):
            ins = None
            for i, (l, r) in enumerate(pairs):
                ins = e.matmul(PS(vb)[:], l, r, start=(i == 0), stop=(i == len(pairs) - 1))
            return ins

        def _e_za(e, vb, pairs):
            ins = None
            for i, (l, r) in enumerate(pairs):
                ins = e.matmul(PS(vb)[:], l, r, start=(i == 0), stop=(i == len(pairs) - 1))
            return ins

        def _e_bc(e, vb, pairs):
            ins = None
            for i, (l, r) in enumerate(pairs):
                ins = e.matmul(PS(vb)[:], l, r, start=(i == 0), stop=(i == len(pairs) - 1))
            return ins

        def _e_out(e, vb, pairs):
            ins = None
            for i, (l, r) in enumerate(pairs):
                ins = e.matmul(PS(vb)[:], l, r, start=(i == 0), stop=(i == len(pairs) - 1))
            return ins

        EMIT = {GF: _e_f, GV: _e_v, GQ: _e_q, GZA: _e_za, "bc": _e_bc, "out": _e_out}

        def mm_group(vb, pairs, reads, n, fp32=False, tag=None):
            em = EMIT.get(tag, _e_c)
            add("pe", lambda e: em(e, vb, pairs), reads, [vb], len(pairs) * _mmc(n, fp32))

        def gn_sums(sq, sqkey, ngrp, eps_n, sl, dgs):
            vb = P.vb()
            ncol = 4 * ngrp

            def fn(e):
                ins = None
                for i in range(4):
                    ins = e.matmul(PS(vb)[:, i * ngrp:(i + 1) * ngrp], sq[:, i * 128:(i + 1) * 128],
                                   (ones[:, 0:1] if ngrp == 1 else ind2), start=True, stop=True)
                return ins
            add("pe", fn, [sqkey, "cb"], [vb], 4 * 110.0)
            add("dve", lambda e: e.tensor_scalar(
                out=rc[sl][:, 0:ncol], in0=PS(vb)[:, 0:ncol], scalar1=float(eps_n), scalar2=None,
                op0=ALU.add), [vb], [("rc", sl)], 200)
            add("pool", lambda e: e.tensor_tensor(
                out=rc[sl][:, 0:ncol], in0=rc[sl][:, 0:ncol],
                in1=pd[:, 28:29].to_broadcast([128, ncol]), op=ALU.pow),
                [("rc", sl), "mh"], [("rc", sl)], 900)
            add("pool", lambda e: e.tensor_copy(out=rh[sl][:, 0:ncol], in_=rc[sl][:, 0:ncol]),
                [("rc", sl)], [("rh", sl)], 300)
            add("pool", lambda e: e.tensor_tensor(out=rl[sl][:, 0:ncol], in0=rc[sl][:, 0:ncol],
                                                  in1=rh[sl][:, 0:ncol], op=ALU.subtract),
                [("rc", sl), ("rh", sl)], [("rl", sl)], 550)

        def gn_bc(ngrp, sl):
            vb = P.vb()

            def fn(e):
                ins = None
                for i in range(4):
                    for g in range(ngrp):
                        pr = slice(0, 128) if ngrp == 1 else slice(g * 64, (g + 1) * 64)
                        m = 128 // ngrp
                        for k, src in enumerate((rh, rl)):
                            c = i * ngrp + g
                            ins = e.matmul(PS(vb)[pr, i * 128:(i + 1) * 128],
                                           src[sl][:, c:c + 1].to_broadcast([128, m]), ident,
                                           start=(k == 0), stop=(k == 1))
                return ins
            add("pe", fn, [("rh", sl), ("rl", sl), "cbi"], [vb], 8 * ngrp * _mmc(128))
            return vb

        def stage_X(j):
            jb = j % 2
            for i in range(4):
                ti = 4 * j + i
                s_ = alloc_xslot()
                xs = ti % 2
                if ti not in xpre:
                    load_x(ti, s_)
                add("act", lambda e, s_=s_, ti=ti, xs=xs: e.activation(
                    out=xn[xs][:], in_=xt[s_][:], func=AF.Square, accum_out=ss[:, ti:ti + 1]),
                    [("xt", s_), "ss"], [("ss", ti), ("xn", xs)], _actc(1024))
                if j == 0:
                    add("act", lambda e, ti=ti: e.activation(
                        out=rstd[:, ti:ti + 1], in_=ss[:, ti:ti + 1], func=AF.Ln, bias=epsb[:, 0:1],
                        scale=1.0 / D),
                        [("ss", ti), "epsb"], [("rstd", ti)], 300).tset = "B"
                    add("act", lambda e, ti=ti: e.activation(
                        out=rstd[:, ti:ti + 1], in_=rstd[:, ti:ti + 1], func=AF.Exp, scale=-0.5),
                        [("rstd", ti)], [("rstd", ti)], 300).tset = "B"
                else:
                    add("pool", lambda e, ti=ti: e.tensor_scalar(
                        out=rstd[:, ti:ti + 1], in0=ss[:, ti:ti + 1], scalar1=1.0 / D,
                        scalar2=float(EPS), op0=ALU.mult, op1=ALU.add), [("ss", ti)],
                        [("rstd", ti)], 500)
                    add("pool", lambda e, ti=ti: e.tensor_tensor(
                        out=rstd[:, ti:ti + 1], in0=rstd[:, ti:ti + 1], in1=pd[:, 28:29],
                        op=ALU.pow), [("rstd", ti), "mh"], [("rstd", ti)], 700)
                add("dve", lambda e, s_=s_, ti=ti, xs=xs: e.scalar_tensor_tensor(
                    out=xn[xs][:], in0=xt[s_][:], scalar=rstd[:, ti:ti + 1], in1=gbc[:],
                    op0=ALU.mult, op1=ALU.mult),
                    [("xt", s_), ("rstd", ti), "gbc"], [("xn", xs)], _dvec(1024))
                tv = P.vb("tr")

                def fn(e, tv=tv, xs=xs):
                    ins = None
                    for dch in range(8):
                        ins = e.transpose(TR(tv)[:, dch * 128:(dch + 1) * 128],
                                          xn[xs][:, dch * 128:(dch + 1) * 128], ident)
                    return ins
                add("pe", fn, [("xn", xs), "cbi"], [tv], 8 * 100.0)
                dst = hT[jb][:, :, i * 128:(i + 1) * 128]
                add("act", lambda e, tv=tv, dst=dst: e.copy(
                    out=dst, in_=TR(tv)[:, :].rearrange("p (q t) -> p q t", q=8)),
                    [tv], [("hT", jb, i, 0), ("hT", jb, i, 1)], _actc(1024))

        def hT_keys(jb, tiles=range(4)):
            return [("hT", jb, i, hf) for i in tiles for hf in range(2)]

        def inproj_fm(j, g, sub):
            jb = j % 2
            vb = P.vb()
            c0 = g * 512 + sub * 128
            pairs = [(win[:, k, c0:c0 + 128], hT[jb][:, k, :]) for k in range(8)]
            mm_group(vb, pairs, [("win", g, sub) if g == GF else ("win", g)] + hT_keys(jb), 512, tag=g)
            return vb

        def stage_F(j):
            for h in range(4):
                vb = inproj_fm(j, GF, h)
                add("act", lambda e, vb=vb, h=h: e.activation(out=tt[h][:], in_=PS(vb)[:],
                                                               func=AF.Tanh, scale=0.5),
                    [vb], [("tt", h)], _actc(512)).tset = "A"

        def stage_V(j):
            jb = j % 2
            for i in range(4):
                vb = P.vb()
                pairs = [(hT[jb][:, k, i * 128:(i + 1) * 128], win[:, k, GV * 512:(GV + 1) * 512])
                         for k in range(8)]
                mm_group(vb, pairs, [("win", GV)] + hT_keys(jb, [i]), 512, tag=GV)
                add("act", lambda e, vb=vb, i=i: e.copy(out=vbf[:, i, :], in_=PS(vb)[:]),
                    [vb], [("vbf", i)], _actc(512))

        def stage_CH(j):
            for h in range(4):
                ls = h % 2
                add("act", lambda e, h=h, ls=ls: e.activation(
                    out=LL[ls][:], in_=tt[h][:], func=AF.Ln, scale=pd[:, 8 + h:9 + h],
                    bias=pd[:, 16 + h:17 + h]), [("tt", h), "pd2", "pd4"], [("LL", ls)],
                    _actc(512)).tset = "B"
                add("dve", lambda e, h=h, ls=ls: e.tensor_tensor_scan(
                    out=bb[h][:], data0=scanmask, data1=LL[ls][:], initial=0.0,
                    op0=ALU.mult, op1=ALU.add), [("LL", ls), "cb"], [("bb", h)], 1250)
                add("act", lambda e, h=h: e.activation(
                    out=decb[:, h, 8 * j:8 * j + 8], in_=bb[h][:, 63:TB:64], func=AF.Exp),
                    [("bb", h)], [("dec", h, j)], 250).tset = "B"
                add("act", lambda e, h=h, ls=ls: e.activation(out=LL[ls][:], in_=bb[h][:],
                                                              func=AF.Exp, scale=-1.0),
                    [("bb", h)], [("LL", ls)], _actc(512)).tset = "B"
                add("act", lambda e, h=h: e.activation(out=bb[h][:], in_=bb[h][:], func=AF.Exp),
                    [("bb", h)], [("bb", h)], _actc(512)).tset = "B"
                add("dve", lambda e, h=h, ls=ls: e.scalar_tensor_tensor(
                    out=kinvT[:, h, :], in0=tt[h][:], scalar=1.0, in1=LL[ls][:],
                    op0=ALU.subtract, op1=ALU.mult), [("tt", h), ("LL", ls)], [("kinvT", h)],
                    _dvec(512))

        def stage_Q(j):
            for h in range(4):
                vb = inproj_fm(j, GQ, h)
                add("dve", lambda e, vb=vb, h=h: e.scalar_tensor_tensor(
                    out=qdec[:, h, :], in0=PS(vb)[:], scalar=pd[:, 12 + h:13 + h], in1=bb[h][:],
                    op0=ALU.mult, op1=ALU.mult), [vb, ("bb", h), "pd3"], [("qdec", h)],
                    _dvec(512, True))

        def stage_C(j, c):
            s_ = c % 2
            vb = inproj_fm(j, GU, c)
            add("act", lambda e, vb=vb: e.copy(out=usb[0][:], in_=PS(vb)[:]),
                [vb], [("usb", 0)], _actc(512))
            vb = inproj_fm(j, GGC, c)
            add("dve", lambda e, vb=vb: e.tensor_tensor(
                out=cu[c][:, 2:TB + 2], in0=PS(vb)[:], in1=usb[0][:], op=ALU.mult),
                [vb, ("usb", 0)], [("cub", c)], _dvec(512, True))
            add("dve", lambda e: e.tensor_scalar(
                out=ct[s_][:], in0=cu[c][:, 2:TB + 2], scalar1=pp[:, 8 + 8 + c:8 + 8 + c + 1],
                scalar2=None, op0=ALU.mult), [("cub", c), "pp"], [("ct", s_)],
                _dvec(512, fast=True))
            for (lo, col) in ((1, 8 + 4 + c), (0, 8 + c)):
                add("dve", lambda e, lo=lo, col=col: e.scalar_tensor_tensor(
                    out=ct[s_][:], in0=cu[c][:, lo:TB + lo], scalar=pp[:, col:col + 1],
                    in1=ct[s_][:], op0=ALU.mult, op1=ALU.add),
                    [("cub", c), ("cuh", c), ("ct", s_), "pp"], [("ct", s_)], _dvec(512))
            add("dve", lambda e: e.tensor_copy(out=cu[c][:, 0:2], in_=cu[c][:, TB:TB + 2]),
                [("cub", c)], [("cuh", c)], 120)
            vb = inproj_fm(j, GGB, c)
            add("dve", lambda e, vb=vb: e.tensor_tensor(
                out=ct[s_][:], in0=PS(vb)[:], in1=ct[s_][:], op=ALU.mult),
                [vb, ("ct", s_)], [("ct", s_)], _dvec(512, True))
            add("act", lambda e: e.activation(out=osq[s_][:], in_=ct[s_][:], func=AF.Square),
                [("ct", s_)], [("osq", s_)], _actc(512))
            vbz = inproj_fm(j, GZB, c)
            add("act", lambda e: e.activation(out=sa[s_][:], in_=PS(vbz)[:], func=AF.Silu),
                [vbz], [("sa", s_)], _actc(512)).tset = "A"
            gn_sums(osq[s_], ("osq", s_), 2, 64 * EPS, s_, (0, 1))
            vb3 = gn_bc(2, s_)
            add("dve", lambda e: e.scalar_tensor_tensor(
                out=ct[s_][:], in0=ct[s_][:], scalar=pd[:, 24 + c:25 + c], in1=PS(vb3)[:],
                op0=ALU.mult, op1=ALU.mult), [("ct", s_), vb3, "pd6"], [("ct", s_)],
                _dvec(512, True))
            add("dve", lambda e: e.tensor_tensor(
                out=mixT[:, 4 + c, :], in0=ct[s_][:], in1=sa[s_][:], op=ALU.mult),
                [("ct", s_), ("sa", s_)], [("mixT", 4 + c)], _dvec(512))

        def stage_KT(j):
            for i in range(4):
                tv = P.vb("tr")

                def fn(e, i=i, tv=tv):
                    ins = None
                    for h in range(4):
                        ins = e.transpose(TR(tv)[:, h * 128:(h + 1) * 128],
                                          kinvT[:, h, i * 128:(i + 1) * 128], ident)
                    return ins
                add("pe", fn, [("kinvT", h) for h in range(4)] + ["cbi"], [tv], 4 * 100.0)
                add("act", lambda e, i=i, tv=tv: e.copy(out=ktok[:, i, :], in_=TR(tv)[:, 0:512]),
                    [tv], [("ktok", i)], _actc(512))

        def stage_SC(j):
            for i in range(4):
                vb = P.vb()

                def fn(e, i=i, vb=vb):
                    ins = None
                    for h in range(4):
                        ins = e.matmul(PS(vb)[:, h * 128:(h + 1) * 128],
                                       kinvT[:, h, i * 128:(i + 1) * 128],
                                       qdec[:, h, i * 128:(i + 1) * 128], start=True, stop=True)
                    return ins
                add("pe", fn, [("kinvT", h) for h in range(4)] + [("qdec", h) for h in range(4)],
                    [vb], 4 * _mmc(128))
                add("dve", lambda e, i=i, vb=vb: e.tensor_tensor(
                    out=scm[:, i, :].rearrange("p (h t) -> p h t", h=4),
                    in0=PS(vb)[:, :].rearrange("p (h t) -> p h t", h=4),
                    in1=mask1.unsqueeze(1).to_broadcast([128, 4, 128]), op=ALU.mult),
                    [vb, "cb"], [("scm", i)], _dvec(512, True))

        def stage_U(j, hp):
            heads = (2 * hp, 2 * hp + 1)
            for n in range(8):
                gn = 8 * j + n
                i = n // 2
                po = (n % 2) * 64
                vb = P.vb()

                def fn(e, i=i, po=po, vb=vb):
                    ins = None
                    for h in heads:
                        ins = e.matmul(PS(vb)[:, h * 128:(h + 1) * 128],
                                       ktok[po:po + 64, i, h * 128:(h + 1) * 128],
                                       vbf[po:po + 64, i, h * 128:(h + 1) * 128],
                                       start=True, stop=True)
                    return ins
                add("pe", fn, [("ktok", i), ("vbf", i)], [vb], 2 * _mmc(128))
                for h in heads:
                    hs = slice(h * 128, (h + 1) * 128)
                    hl = slice((h % 2) * 128, (h % 2 + 1) * 128)
                    if gn == 0:
                        add("dve", lambda e, vb=vb, hs=hs: e.tensor_copy(out=Tst[:, hs],
                                                                       in_=PS(vb)[:, hs]),
                            [vb], [("T", h)], _dvec(128, True))
                    else:
                        add("dve", lambda e, vb=vb, hs=hs, h=h, gn=gn: e.scalar_tensor_tensor(
                            out=Tst[:, hs], in0=Tst[:, hs], scalar=decb[:, h, gn - 1:gn],
                            in1=PS(vb)[:, hs], op0=ALU.mult, op1=ALU.add),
                            [vb, ("T", h), ("dec", h, (gn - 1) // 8)], [("T", h)],
                            _dvec(128, True) + 100)
                    if n < 7:
                        dst, dkey = Sbf[:, n, hl], ("Sbf", n, h % 2)
                    else:
                        dst, dkey = Sb0[:, (j + 1) % 2, hs], ("Sb0", (j + 1) % 2, h)
                    add("act", lambda e, hs=hs, h=h, gn=gn, dst=dst: e.mul(
                        out=dst, in_=Tst[:, hs], mul=decb[:, h, gn:gn + 1]),
                        [("T", h), ("dec", h, gn // 8)], [dkey], _actc(128) + 100)

        def stage_O(j, h):
            hs = slice(h * 128, (h + 1) * 128)
            s_ = h % 2
            vb = P.vb()

            def fn(e):
                ins = None
                for i in range(4):
                    ins = e.matmul(PS(vb)[:, i * 128:(i + 1) * 128], vbf[:, i, hs],
                                   scm[:, i, hs], start=True, stop=False)
                    for n in (2 * i, 2 * i + 1):
                        gn = 8 * j + n
                        if gn == 0:
                            continue
                        ins = e.matmul(PS(vb)[:, n * 64:(n + 1) * 64],
                                       (Sb0[:, j % 2, hs] if n == 0 else
                                        Sbf[:, n - 1, (h % 2) * 128:(h % 2 + 1) * 128]),
                                       qdec[:, h, n * 64:(n + 1) * 64],
                                       start=False, stop=(n == 2 * i + 1))
                return ins
            add("pe", fn,
                [("vbf", i) for i in range(4)] + [("scm", i) for i in range(4)]
                + [("Sbf", n, h % 2) for n in range(7)] + [("Sb0", j % 2, h)] + [("qdec", h)],
                [vb], 4 * _mmc(128) + 8 * _mmc(64))
            add("act", lambda e: e.activation(out=osq[s_][:], in_=PS(vb)[:], func=AF.Square),
                [vb], [("osq", s_)], _actc(512))
            vbz = inproj_fm(j, GZA, h)
            add("act", lambda e: e.activation(out=sa[s_][:], in_=PS(vbz)[:], func=AF.Silu),
                [vbz], [("sa", s_)], _actc(512)).tset = "A"
            gn_sums(osq[s_], ("osq", s_), 1, 128 * EPS, s_, (s_,))
            vb3 = gn_bc(1, s_)
            add("act", lambda e: e.copy(out=rs[s_][:], in_=PS(vb3)[:]),
                [vb3], [("rs", s_)], _actc(512))
            add("dve", lambda e: e.scalar_tensor_tensor(
                out=rs[s_][:], in0=PS(vb)[:], scalar=pd[:, 20 + h:21 + h], in1=rs[s_][:],
                op0=ALU.mult, op1=ALU.mult), [vb, ("rs", s_), "pd5"], [("rs", s_)],
                _dvec(512, True))
            add("dve", lambda e: e.tensor_tensor(
                out=mixT[:, h, :], in0=rs[s_][:], in1=sa[s_][:], op=ALU.mult),
                [("rs", s_), ("sa", s_)], [("mixT", h)], _dvec(512))

        def stage_OUT(j):
            for i in range(4):
                ti = 4 * j + i
                s_ = alloc_xslot()
                xs = ti % 2
                load_x(ti, s_)
                for hf in range(2):
                    vb = P.vb()
                    pairs = [(mixT[:, k, i * 128:(i + 1) * 128], wout[:, k, hf * 512:(hf + 1) * 512])
                             for k in range(8)]
                    mm_group(vb, pairs, [("mixT", k) for k in range(8)] + [("wout", hf)], 512, tag="out")
                    add("dve", lambda e, vb=vb, s_=s_, hf=hf: e.tensor_tensor(
                        out=xt[s_][:, hf * 512:(hf + 1) * 512], in0=PS(vb)[:],
                        in1=xt[s_][:, hf * 512:(hf + 1) * 512], op=ALU.add),
                        [vb, ("xt", s_)], [("xt", s_)], _dvec(512, True))
                add("act", lambda e, s_=s_, ti=ti, xs=xs: e.activation(
                    out=xn[xs][:], in_=xt[s_][:], func=AF.Square, accum_out=ss2[:, ti:ti + 1]),
                    [("xt", s_), "ss2"], [("ss2", ti), ("xn", xs)], _actc(1024))
                add("pool", lambda e, ti=ti: e.tensor_scalar(
                    out=rstd2[:, ti:ti + 1], in0=ss2[:, ti:ti + 1], scalar1=1.0 / D,
                    scalar2=float(EPS), op0=ALU.mult, op1=ALU.add), [("ss2", ti)],
                    [("rstd2", ti)], 500)
                add("pool", lambda e, ti=ti: e.tensor_tensor(
                    out=rstd2[:, ti:ti + 1], in0=rstd2[:, ti:ti + 1], in1=pd[:, 28:29], op=ALU.pow),
                    [("rstd2", ti), "mh"], [("rstd2", ti)], 700)
                add("dve", lambda e, s_=s_, ti=ti: e.scalar_tensor_tensor(
                    out=xt[s_][:], in0=xt[s_][:], scalar=rstd2[:, ti:ti + 1], in1=fgbc[:],
                    op0=ALU.mult, op1=ALU.mult),
                    [("xt", s_), ("rstd2", ti), "fgbc"], [("xt", s_)], _dvec(1024))
                dma("sp", out_d[ti * 128:(ti + 1) * 128, :], xt[s_][:], [("xt", s_)], [("out", ti)],
                    524288)

        stage_X(0)
        for j in range(NB):
            stage_F(j)
            stage_V(j)
            stage_CH(j)
            stage_Q(j)
            stage_C(j, 0)
            stage_C(j, 1)
            stage_KT(j)
            stage_SC(j)
            stage_U(j, 0)
            stage_C(j, 2)
            if j + 1 < NB:
                stage_X(j + 1)
            stage_C(j, 3)
            stage_O(j, 0)
            stage_O(j, 1)
            stage_U(j, 1)
            stage_O(j, 2)
            stage_O(j, 3)
            stage_OUT(j)

        P.add("sp", lambda e: e.nop(), [("out", ti) for ti in range(NT)], [], cost=50)

        P.schedule({"ps": list(range(NPS)), "tr": list(range(NTR))})
        P.finalize()
        print("sbuf bytes remaining:", nc.sbuf_bytes_remaining,
              {e: len(P.ops[e]) for e in P.ENGS}, "est makespan us: %.1f" % (P.makespan / 1e3))
        with nc.Block() as block:
            P.emit(nc, block, sems, dsems)
    return nc


def _consts():
    cb = np.zeros((128, 1280), np.float32)
    s = np.arange(128)[:, None]
    t = np.arange(128)[None, :]
    m = ((s // 64) == (t // 64)) & (s <= t)
    cb[:, 384:512] = m.astype(np.float32)
    sm = np.ones((512,), np.float32)
    sm[0::64] = 0.0
    cb[:, 512:1024] = sm[None, :]
    cb[:, 0:128] = np.eye(128, dtype=np.float32)
    cb[:, 128:256] = 1.0
    cb[:, 256] = (np.arange(128) < 64)
    cb[:, 257] = (np.arange(128) >= 64)
    cb[:, 1024 + 64:1024 + 192] = 1.0
    return cb


_NC_CACHE = {}


def kernel(x, norm_gain, w_in, lb_logits, conv_w, hgrn_norm_gain, conv_norm_gain, w_out,
           final_norm_gain):
    x = np.asarray(x, np.float32)
    B = x.shape[0]
    w_in2 = np.ascontiguousarray(np.asarray(w_in, np.float32)[0])
    w_out2 = np.ascontiguousarray(np.asarray(w_out, np.float32)[0])
    lb = np.asarray(lb_logits, np.float32)
    cw = np.asarray(conv_w, np.float32)[0]
    hg = np.asarray(hgrn_norm_gain, np.float32)[0]
    cg = np.asarray(conv_norm_gain, np.float32)[0]
    pp = np.zeros((128, 32), np.float32)
    pp[:, 0:4] = lb[0].reshape(4, 128).T
    pp[:, 4:8] = lb[1].reshape(4, 128).T
    for jj in range(3):
        pp[:, 8 + 4 * jj: 12 + 4 * jj] = cw[jj].reshape(4, 128).T
    pp[:, 20:24] = hg.reshape(4, 128).T
    pp[:, 24:28] = cg.reshape(4, 128).T
    cb = _consts()
    ng = np.ascontiguousarray(np.asarray(norm_gain, np.float32).reshape(1, D))
    fg = np.ascontiguousarray(np.asarray(final_norm_gain, np.float32).reshape(1, D))
    if "nc" not in _NC_CACHE:
        _NC_CACHE["nc"] = build_nc()
    nc = _NC_CACHE["nc"]
    in_maps = [{"x": np.ascontiguousarray(x[b]), "w_in": w_in2, "w_out": w_out2, "pp": pp,
                "cb": cb, "ng": ng, "fg": fg} for b in range(B)]
    res = run_bass_kernel_spmd(nc, in_maps, core_ids=list(range(B)))
    return np.stack([np.asarray(r["out"]) for r in res.results], axis=0).astype(np.float32)
```

```python
import contextlib
import numpy as np
import concourse.bass as bass
import concourse.mybir as mybir
from concourse.bass_utils import run_bass_kernel_spmd

F32 = mybir.dt.float32
BF16 = mybir.dt.bfloat16
AF = mybir.ActivationFunctionType
ALU = mybir.AluOpType

S = 2048
D = 1024
NCOL = 4096
TB = 512
NB = S // TB
NT = S // 128
EPS = 1e-6
GQ, GF, GV, GZA, GU, GGB, GGC, GZB = range(8)
NPS = 6
NTR = 2
NXT = 4
NDMASEM = {"sp": 16, "pool": 16, "act": 4}
HOP = 600.0
WINDOW = 300


class VB:
    __slots__ = ("pool", "p", "ops", "idx")

    def __init__(self, pool):
        self.pool = pool
        self.p = None
        self.ops = []
        self.idx = -1


class _Op:
    pass


class Prog:
    ENGS = ("pe", "act", "dve", "pool", "sp")

    def __init__(self):
        self.ops = {e: [] for e in self.ENGS}
        self.all = []
        self.lastw = {}
        self.readers = {}
        self.dmas = {e: [] for e in self.ENGS}
        self.vbs = {}

    def vb(self, pool="ps"):
        v = VB(pool)
        lst = self.vbs.setdefault(pool, [])
        v.idx = len(lst)
        lst.append(v)
        return v

    def add(self, eng, fn, reads=(), writes=(), cost=500.0, dma=False):
        op = _Op()
        op.eng, op.fn, op.dma, op.cost = eng, fn, dma, float(cost)
        op.gidx = len(self.all)
        op.sig = False
        op.sigval = 0
        op.waits = []
        op.acq = None
        op.vbl = []
        op.tset = None
        deps = []
        for k in reads:
            w = self.lastw.get(k)
            if w is not None:
                deps.append((w, "raw"))
            if isinstance(k, VB):
                for r in self.readers.get(k, ()):
                    if r.eng != eng:
                        deps.append((r, "war"))
        for k in writes:
            w = self.lastw.get(k)
            if w is not None:
                deps.append((w, "waw"))
            for r in self.readers.get(k, ()):
                deps.append((r, "war"))
        for k in list(reads) + list(writes):
            if isinstance(k, VB):
                if not k.ops:
                    assert k in writes, "first access of a psum bank must be a write"
                    op.acq = k
                if not k.ops or k.ops[-1] is not op:
                    k.ops.append(op)
                if k not in op.vbl:
                    op.vbl.append(k)
        for k in reads:
            self.readers.setdefault(k, []).append(op)
        for k in writes:
            self.lastw[k] = op
            self.readers[k] = []
        op.deps = deps
        if dma:
            lst = self.dmas[eng]
            nsem = NDMASEM[eng]
            op.dsem = (eng, len(lst) % nsem)
            op.dval = 16 * (len(lst) // nsem + 1)
            if len(lst) >= nsem:
                deps.append((lst[len(lst) - nsem], "raw"))
            lst.append(op)
        self.all.append(op)
        return op

    def schedule(self, pools):
        ops = self.all
        for op in ops:
            op.succ = []
            op.done = False
            op.start = op.end = 0.0
        for op in ops:
            preds = {id(p): p for (p, _) in op.deps if p is not op}
            op.npred = len(preds)
            for p in preds.values():
                p.succ.append(op)
        for op in reversed(ops):
            op.blevel = op.cost + max((s_.blevel for s_ in op.succ), default=0.0)
        free = {e: 0.0 for e in self.ENGS}
        bank_free = {pl: {b: 0.0 for b in ids} for pl, ids in pools.items()}
        bank_prev = {pl: {b: None for b in ids} for pl, ids in pools.items()}
        ready = [op for op in ops if op.npred == 0]
        pool_dmas = [op for op in ops if op.dma and op.eng == "pool"]
        pool_ptr = [0]
        cur_tset = [None]
        dma_pipe = [0.0]
        nsched = 0
        lowest = 0
        order = {e: [] for e in self.ENGS}
        n = len(ops)
        while nsched < n:
            while lowest < n and ops[lowest].done:
                lowest += 1
            best = None
            for op in ready:
                if op.gidx > lowest + WINDOW:
                    continue
                if op.dma and op.eng == "pool" and pool_dmas[pool_ptr[0]] is not op:
                    continue
                st = free[op.eng]
                for (p, kind) in op.deps:
                    if p is op:
                        continue
                    if p.eng == op.eng and not (op.dma or p.dma):
                        t = p.end if op.eng == "pe" else p.end + 60.0
                    else:
                        t = p.end + HOP
                    if t > st:
                        st = t
                if op.tset is not None and op.tset != cur_tset[0]:
                    st += 2400.0
                bank = None
                if op.acq is not None:
                    vb = op.acq
                    pl = vb.pool
                    cands = []
                    for b in pools[pl]:
                        pv = bank_prev[pl][b]
                        if pv is None or all(o.done for o in pv.ops):
                            cands.append(b)
                    if not cands:
                        continue
                    earlier_unacq = sum(1 for v in self.vbs[pl][:vb.idx] if v.p is None)
                    if earlier_unacq > 0 and len(cands) <= min(2, earlier_unacq):
                        continue
                    bank = min(cands, key=lambda b: bank_free[pl][b])
                    st = max(st, bank_free[pl][bank] + HOP)
                key = (st, 0.0 if op.dma else -op.blevel, op.gidx)
                if best is None or key < best[0]:
                    best = (key, op, bank)
            if best is None:
                raise RuntimeError("scheduler stuck at op %d" % lowest)
            (st, _, _), op, bank = best
            if op.acq is not None:
                vb = op.acq
                pl = vb.pool
                vb.p = bank
                pv = bank_prev[pl][bank]
                if pv is not None:
                    for o in pv.ops:
                        op.deps.append((o, "war"))
                bank_prev[pl][bank] = vb
            if op.tset is not None:
                cur_tset[0] = op.tset
            op.start = st
            op.end = st + op.cost
            if op.dma and getattr(op, "nbytes", 0):
                d0 = max(st + 1500.0, dma_pipe[0])
                dma_pipe[0] = d0 + op.nbytes / 330.0
                op.end = dma_pipe[0] + 500.0
            free[op.eng] = (st + (1100.0 if op.eng == "pool" else 80.0)) if op.dma else op.end
            for k in op.vbl:
                if op.end > bank_free[k.pool][k.p]:
                    bank_free[k.pool][k.p] = op.end
            op.done = True
            if op.dma and op.eng == "pool":
                pool_ptr[0] += 1
            nsched += 1
            ready.remove(op)
            order[op.eng].append(op)
            for s_ in op.succ:
                s_.npred -= 1
                if s_.npred == 0:
                    ready.append(s_)
        for e in self.ENGS:
            self.ops[e] = order[e]
            for i, op in enumerate(order[e]):
                op.idx = i
        self.makespan = max(op.end for op in ops)

    def finalize(self):
        for e in self.ENGS:
            waited = {f: -1 for f in self.ENGS}
            dma_waited = set()
            for op in self.ops[e]:
                need = {}
                dneed = []
                for (p, kind) in op.deps:
                    if p is op:
                        continue
                    if p.dma:
                        if id(p) not in dma_waited:
                            dma_waited.add(id(p))
                            dneed.append(p)
                        continue
                    if p.eng == e and not op.dma:
                        if e == "pe":
                            continue
                    if p.idx <= waited[p.eng]:
                        continue
                    if p.idx > need.get(p.eng, -1):
                        need[p.eng] = p.idx
                for f, idx in need.items():
                    waited[f] = idx
                    prod = self.ops[f][idx]
                    prod.sig = True
                    op.waits.append(("c", prod))
                for p in dneed:
                    op.waits.append(("d", p))
        for e in self.ENGS:
            c = 0
            for op in self.ops[e]:
                if op.sig:
                    c += 1
                op.sigval = c

    def emit(self, nc, block, sems, dsems):
        handles = {"pe": block.tensor, "act": block.scalar, "dve": block.vector,
                   "pool": block.gpsimd, "sp": block.sync}
        for e in self.ENGS:
            ops = self.ops[e]
            if not ops:
                continue

            def body(eng, ops=ops, e=e):
                for op in ops:
                    for (kind, p) in op.waits:
                        if kind == "c":
                            eng.wait_ge(sems[p.eng], p.sigval)
                        else:
                            eng.wait_ge(dsems[p.dsem], p.dval)
                    ins = op.fn(eng)
                    if op.dma:
                        ins.then_inc(dsems[op.dsem], 16)
                    elif op.sig:
                        ins.then_inc(sems[e], 1)

            handles[e](body)


def _mmc(n, fp32=False):
    return (max(n, 64) * (4 if fp32 else 1)) / 1.9 + 20.0


def _actc(n):
    return 240.0 + n / 1.2


def _dvec(n, psum=False, fast=False):
    return (160.0 if psum else 100.0) + n / (1.9 if fast else 0.96)


def build_nc():
    nc = bass.Bass("TRN2", target_bir_lowering=False)
    try:
        nc.allow_low_precision("bf16 matmul operands, fp32 accumulation")
    except Exception:
        pass
    x_d = nc.dram_tensor("x", [S, D], F32, kind="ExternalInput").ap()
    win_d = nc.dram_tensor("w_in", [D, NCOL], F32, kind="ExternalInput").ap()
    wout_d = nc.dram_tensor("w_out", [D, D], F32, kind="ExternalInput").ap()
    pp_d = nc.dram_tensor("pp", [128, 32], F32, kind="ExternalInput").ap()
    cb_d = nc.dram_tensor("cb", [128, 1280], F32, kind="ExternalInput").ap()
    ng_d = nc.dram_tensor("ng", [1, D], F32, kind="ExternalInput").ap()
    fg_d = nc.dram_tensor("fg", [1, D], F32, kind="ExternalInput").ap()
    out_d = nc.dram_tensor("out", [S, D], F32, kind="ExternalOutput").ap()

    P = Prog()
    es = contextlib.ExitStack()
    with es:
        def sb(name, shape, dt):
            return es.enter_context(nc.sbuf_tensor(name, shape, dt))

        def psb(name, shape, dt):
            return es.enter_context(nc.psum_tensor(name, shape, dt))

        win = sb("win", [128, 8, NCOL], BF16)
        wout = sb("wout", [128, 8, D], BF16)
        pp = sb("pp_sb", [128, 32], F32)
        pd = sb("pd_sb", [128, 32], F32)
        cb = sb("cb_sb", [128, 1280], BF16)
        gbc = sb("gbc", [128, D], F32)
        fgbc = sb("fgbc", [128, D], F32)
        xt = [sb(f"xt{i}", [128, D], F32) for i in range(NXT)]
        xn = [sb(f"xn{i}", [128, D], BF16) for i in range(2)]
        ss = sb("ss", [128, NT], F32)
        epsb = sb("epsb", [128, 1], F32)
        rstd = sb("rstd", [128, NT], F32)
        ss2 = sb("ss2", [128, NT], F32)
        rstd2 = sb("rstd2", [128, NT], F32)
        hT = [sb(f"hT{i}", [128, 8, TB], BF16) for i in range(2)]
        tt = [sb(f"tt{h}", [128, TB], F32) for h in range(4)]
        bb = [sb(f"bb{h}", [128, TB], F32) for h in range(4)]
        LL = [sb(f"LL{i}", [128, TB], F32) for i in range(2)]
        decb = sb("decb", [128, 4, S // 64], F32)
        qdec = sb("qdec", [128, 4, TB], BF16)
        kinvT = sb("kinvT", [128, 4, TB], BF16)
        ktok = sb("ktok", [128, 4, 512], BF16)
        vbf = sb("vbf", [128, 4, 512], BF16)
        scm = sb("scm", [128, 4, 512], BF16)
        Sbf = sb("Sbf", [128, 7, 256], BF16)
        Sb0 = sb("Sb0", [128, 2, 512], BF16)
        Tst = sb("Tst", [128, 512], F32)
        sa = [sb(f"sa{i}", [128, TB], F32) for i in range(2)]
        osq = [sb(f"osq{i}", [128, TB], BF16) for i in range(2)]
        rs = [sb(f"rs{i}", [128, TB], F32) for i in range(2)]
        usb = [sb(f"usb{i}", [128, TB], F32) for i in range(1)]
        cu = [sb(f"cu{c}", [128, TB + 2], F32) for c in range(4)]
        ct = [sb(f"ct{i}", [128, TB], F32) for i in range(2)]
        rc = [sb(f"rc{i}", [128, 8], F32) for i in range(2)]
        rh = [sb(f"rh{i}", [128, 8], BF16) for i in range(2)]
        rl = [sb(f"rl{i}", [128, 8], BF16) for i in range(2)]
        mixT = sb("mixT", [128, 8, TB], BF16)
        ps = [psb(f"ps{i}", [128, 512], F32) for i in range(NPS)]
        trs = [psb(f"tr{i}", [128, 1024], BF16) for i in range(NTR)]

        sems = {e: es.enter_context(nc.semaphore(f"s_{e}")) for e in Prog.ENGS}
        dsems = {(e, i): es.enter_context(nc.semaphore(f"d_{e}{i}"))
                 for e in NDMASEM for i in range(NDMASEM[e])}

        ident = cb[:, 0:128]
        ones = cb[:, 128:256]
        ind2 = cb[:, 256:258]
        L1b = cb[:, 1024:1152]
        L0b = cb[:, 1152:1280]
        mask1 = cb[:, 384:512]
        scanmask = cb[:, 512:1024]

        def PS(v):
            return ps[v.p]

        def TR(v):
            return trs[v.p]

        add = P.add

        def dma(eng, out, in_, reads, writes, nbytes):
            op = P.add(eng, lambda e: e.dma_start(out=out, in_=in_), reads, writes,
                       cost=2200.0 + nbytes / 180.0, dma=True)
            op.nbytes = nbytes
            return op

        xpre = {}
        for ti in range(4):
            xpre[ti] = ti % NXT
            dma("sp", xt[ti % NXT][:], x_d[ti * 128:(ti + 1) * 128, :], [], [("xt", ti % NXT)],
                524288)
            if ti == 0:
                dma("sp", gbc[:], ng_d.partition_broadcast(128), [], ["gbc"], 524288)
        dma("pool", cb[:, 0:128], cb_d[:, 0:128], [], ["cbi"], 65536)
        dma("sp", pp[:], pp_d, [], ["pp"], 16384)
        dma("sp", fgbc[:], fg_d.partition_broadcast(128), [], ["fgbc"], 524288)

        xslot = [0]

        def alloc_xslot():
            s_ = xslot[0] % NXT
            xslot[0] += 1
            return s_

        def load_x(ti, slot):
            dma("sp", xt[slot][:], x_d[ti * 128:(ti + 1) * 128, :], [("win", GZB)], [("xt", slot)],
                524288)

        win_v = win_d.rearrange("(k p) c -> p k c", p=128)
        wout_v = wout_d.rearrange("(k p) c -> p k c", p=128)
        for sub in range(4):
            c0 = GF * 512 + sub * 128
            dma("pool", win[:, :, c0:c0 + 128], win_v[:, :, c0:c0 + 128],
                [("xt", 3)] if sub == 0 else [], [("win", GF, sub)], 524288)
            if sub == 0:
                dma("pool", cb[:, 128:1280], cb_d[:, 128:1280], [], ["cb"], 589824)
        for g in (GV, GQ, GU, GGC, GGB, GZB, GZA):
            dma("pool", win[:, :, g * 512:(g + 1) * 512], win_v[:, :, g * 512:(g + 1) * 512],
                [], [("win", g)], 2097152)
        for hf in range(2):
            dma("pool", wout[:, :, hf * 512:(hf + 1) * 512], wout_v[:, :, hf * 512:(hf + 1) * 512],
                [], [("wout", hf)], 2097152)

        add("dve", lambda e: e.memset(ss[:], 0.0), [], ["ss"], 100)
        add("dve", lambda e: e.memset(epsb[:], float(EPS)), [], ["epsb"], 100)
        add("dve", lambda e: e.memset(ss2[:], 0.0), [], ["ss2"], 100)
        for c in range(4):
            add("dve", lambda e, c=c: e.memset(cu[c][:, 0:2], 0.0), [], [("cuh", c)], 100)

        add("dve", lambda e: e.tensor_tensor(out=pd[:, 0:4], in0=pp[:, 4:8], in1=pp[:, 0:4],
                                             op=ALU.subtract), ["pp"], ["pd0"], 150)
        add("act", lambda e: e.activation(out=pd[:, 0:4], in_=pd[:, 0:4], func=AF.Exp),
            ["pd0"], ["pd0"], 2000)
        add("dve", lambda e: e.memset(pd[:, 28:29], -0.5), [], ["mh"], 100)
        add("dve", lambda e: e.tensor_scalar(out=pd[:, 0:4], in0=pd[:, 0:4], scalar1=1.0,
                                             scalar2=None, op0=ALU.add), ["pd0"], ["pd0"], 150)
        add("dve", lambda e: e.reciprocal(out=pd[:, 4:8], in_=pd[:, 0:4]), ["pd0"], ["pd1"], 150)
        add("dve", lambda e: e.tensor_scalar(out=pd[:, 8:12], in0=pd[:, 4:8], scalar1=-0.5,
                                             scalar2=0.5, op0=ALU.mult, op1=ALU.add),
            ["pd1"], ["pd2"], 150)
        add("dve", lambda e: e.tensor_scalar(out=pd[:, 12:16], in0=pd[:, 8:12], scalar1=-1.0,
                                             scalar2=None, op0=ALU.mult), ["pd2"], ["pd3"], 150)
        add("dve", lambda e: e.tensor_scalar(out=pd[:, 16:20], in0=pd[:, 8:12], scalar1=-1.0,
                                             scalar2=1.0, op0=ALU.mult, op1=ALU.add),
            ["pd2"], ["pd4"], 150)
        add("dve", lambda e: e.tensor_scalar(out=pd[:, 20:24], in0=pp[:, 20:24],
                                             scalar1=float(np.sqrt(128.0)), scalar2=None,
                                             op0=ALU.mult), ["pp"], ["pd5"], 150)
        add("dve", lambda e: e.tensor_scalar(out=pd[:, 24:28], in0=pp[:, 24:28], scalar1=8.0,
                                             scalar2=None, op0=ALU.mult), ["pp"], ["pd6"], 150)
        def _mk(tag):
            def emit(e, vb, pairs):
                k = len(pairs)
                ins = None
                for i, (l, r) in enumerate(pairs):
                    ins = e.matmul(PS(vb)[:], l, r, start=(i == 0), stop=(i == k - 1))
                return ins
            return emit

        def _e_f(e, vb, pairs):
            ins = None
            for i, (l, r) in enumerate(pairs):
                ins = e.matmul(PS(vb)[:], l, r, start=(i == 0), stop=(i == len(pairs) - 1))
            return ins

        def _e_v(e, vb, pairs):
            ins = None
            for i, (l, r) in enumerate(pairs):
                ins = e.matmul(PS(vb)[:], l, r, start=(i == 0), stop=(i == len(pairs) - 1))
            return ins

        def _e_q(e, vb, pairs):
            ins = None
            for i, (l, r) in enumerate(pairs):
                ins = e.matmul(PS(vb)[:], l, r, start=(i == 0), stop=(i == len(pairs) - 1))
            return ins

        def _e_c(e, vb, pairs):
            ins = None
            for i, (l, r) in enumerate(pairs):
                ins = e.matmul(PS(vb)[:], l, r, start=(i == 0), stop=(i == len(pairs) - 1))
            return ins

        def _e_za(e, vb, pairs):
            ins = None
            for i, (l, r) in enumerate(pairs):
                ins = e.matmul(PS(vb)[:], l, r, start=(i == 0), stop=(i == len(pairs) - 1))
            return ins

        def _e_bc(e, vb, pairs):
            ins = None
            for i, (l, r) in enumerate(pairs):
                ins = e.matmul(PS(vb)[:], l, r, start=(i == 0), stop=(i == len(pairs) - 1))
            return ins

        def _e_out(e, vb, pairs):
            ins = None
            for i, (l, r) in enumerate(pairs):
                ins = e.matmul(PS(vb)[:], l, r, start=(i == 0), stop=(i == len(pairs) - 1))
            return ins

        EMIT = {GF: _e_f, GV: _e_v, GQ: _e_q, GZA: _e_za, "bc": _e_bc, "out": _e_out}

        def mm_group(vb, pairs, reads, n, fp32=False, tag=None):
            em = EMIT.get(tag, _e_c)
            add("pe", lambda e: em(e, vb, pairs), reads, [vb], len(pairs) * _mmc(n, fp32))

        def gn_sums(sq, sqkey, ngrp, eps_n, sl, dgs):
            vb = P.vb()
            ncol = 4 * ngrp

            def fn(e):
                ins = None
                for i in range(4):
                    ins = e.matmul(PS(vb)[:, i * ngrp:(i + 1) * ngrp], sq[:, i * 128:(i + 1) * 128],
                                   (ones[:, 0:1] if ngrp == 1 else ind2), start=True, stop=True)
                return ins
            add("pe", fn, [sqkey, "cb"], [vb], 4 * 110.0)
            add("dve", lambda e: e.tensor_scalar(
                out=rc[sl][:, 0:ncol], in0=PS(vb)[:, 0:ncol], scalar1=float(eps_n), scalar2=None,
                op0=ALU.add), [vb], [("rc", sl)], 200)
            add("pool", lambda e: e.tensor_tensor(
                out=rc[sl][:, 0:ncol], in0=rc[sl][:, 0:ncol],
                in1=pd[:, 28:29].to_broadcast([128, ncol]), op=ALU.pow),
                [("rc", sl), "mh"], [("rc", sl)], 900)
            add("pool", lambda e: e.tensor_copy(out=rh[sl][:, 0:ncol], in_=rc[sl][:, 0:ncol]),
                [("rc", sl)], [("rh", sl)], 300)
            add("pool", lambda e: e.tensor_tensor(out=rl[sl][:, 0:ncol], in0=rc[sl][:, 0:ncol],
                                                  in1=rh[sl][:, 0:ncol], op=ALU.subtract),
                [("rc", sl), ("rh", sl)], [("rl", sl)], 550)

        def gn_bc(ngrp, sl):
            vb = P.vb()

            def fn(e):
                ins = None
                for i in range(4):
                    for g in range(ngrp):
                        pr = slice(0, 128) if ngrp == 1 else slice(g * 64, (g + 1) * 64)
                        m = 128 // ngrp
                        for k, src in enumerate((rh, rl)):
                            c = i * ngrp + g
                            ins = e.matmul(PS(vb)[pr, i * 128:(i + 1) * 128],
                                           src[sl][:, c:c + 1].to_broadcast([128, m]), ident,
                                           start=(k == 0), stop=(k == 1))
                return ins
            add("pe", fn, [("rh", sl), ("rl", sl), "cbi"], [vb], 8 * ngrp * _mmc(128))
            return vb

        def stage_X(j):
            jb = j % 2
            for i in range(4):
                ti = 4 * j + i
                s_ = alloc_xslot()
                xs = ti % 2
                if ti not in xpre:
                    load_x(ti, s_)
                add("act", lambda e, s_=s_, ti=ti, xs=xs: e.activation(
                    out=xn[xs][:], in_=xt[s_][:], func=AF.Square, accum_out=ss[:, ti:ti + 1]),
                    [("xt", s_), "ss"], [("ss", ti), ("xn", xs)], _actc(1024))
                if j == 0:
                    add("act", lambda e, ti=ti: e.activation(
                        out=rstd[:, ti:ti + 1], in_=ss[:, ti:ti + 1], func=AF.Ln, bias=epsb[:, 0:1],
                        scale=1.0 / D),
                        [("ss", ti), "epsb"], [("rstd", ti)], 300).tset = "B"
                    add("act", lambda e, ti=ti: e.activation(
                        out=rstd[:, ti:ti + 1], in_=rstd[:, ti:ti + 1], func=AF.Exp, scale=-0.5),
                        [("rstd", ti)], [("rstd", ti)], 300).tset = "B"
                else:
                    add("pool", lambda e, ti=ti: e.tensor_scalar(
                        out=rstd[:, ti:ti + 1], in0=ss[:, ti:ti + 1], scalar1=1.0 / D,
                        scalar2=float(EPS), op0=ALU.mult, op1=ALU.add), [("ss", ti)],
                        [("rstd", ti)], 500)
                    add("pool", lambda e, ti=ti: e.tensor_tensor(
                        out=rstd[:, ti:ti + 1], in0=rstd[:, ti:ti + 1], in1=pd[:, 28:29],
                        op=ALU.pow), [("rstd", ti), "mh"], [("rstd", ti)], 700)
                add("dve", lambda e, s_=s_, ti=ti, xs=xs: e.scalar_tensor_tensor(
                    out=xn[xs][:], in0=xt[s_][:], scalar=rstd[:, ti:ti + 1], in1=gbc[:],
                    op0=ALU.mult, op1=ALU.mult),
                    [("xt", s_), ("rstd", ti), "gbc"], [("xn", xs)], _dvec(1024))
                tv = P.vb("tr")

                def fn(e, tv=tv, xs=xs):
                    ins = None
                    for dch in range(8):
                        ins = e.transpose(TR(tv)[:, dch * 128:(dch + 1) * 128],
                                          xn[xs][:, dch * 128:(dch + 1) * 128], ident)
                    return ins
                add("pe", fn, [("xn", xs), "cbi"], [tv], 8 * 100.0)
                dst = hT[jb][:, :, i * 128:(i + 1) * 128]
                add("act", lambda e, tv=tv, dst=dst: e.copy(
                    out=dst, in_=TR(tv)[:, :].rearrange("p (q t) -> p q t", q=8)),
                    [tv], [("hT", jb, i, 0), ("hT", jb, i, 1)], _actc(1024))

        def hT_keys(jb, tiles=range(4)):
            return [("hT", jb, i, hf) for i in tiles for hf in range(2)]

        def inproj_fm(j, g, sub):
            jb = j % 2
            vb = P.vb()
            c0 = g * 512 + sub * 128
            pairs = [(win[:, k, c0:c0 + 128], hT[jb][:, k, :]) for k in range(8)]
            mm_group(vb, pairs, [("win", g, sub) if g == GF else ("win", g)] + hT_keys(jb), 512, tag=g)
            return vb

        def stage_F(j):
            for h in range(4):
                vb = inproj_fm(j, GF, h)
                add("act", lambda e, vb=vb, h=h: e.activation(out=tt[h][:], in_=PS(vb)[:],
                                                               func=AF.Tanh, scale=0.5),
                    [vb], [("tt", h)], _actc(512)).tset = "A"

        def stage_V(j):
            jb = j % 2
            for i in range(4):
                vb = P.vb()
                pairs = [(hT[jb][:, k, i * 128:(i + 1) * 128], win[:, k, GV * 512:(GV + 1) * 512])
                         for k in range(8)]
                mm_group(vb, pairs, [("win", GV)] + hT_keys(jb, [i]), 512, tag=GV)
                add("act", lambda e, vb=vb, i=i: e.copy(out=vbf[:, i, :], in_=PS(vb)[:]),
                    [vb], [("vbf", i)], _actc(512))

        def stage_CH(j):
            for h in range(4):
                ls = h % 2
                add("act", lambda e, h=h, ls=ls: e.activation(
                    out=LL[ls][:], in_=tt[h][:], func=AF.Ln, scale=pd[:, 8 + h:9 + h],
                    bias=pd[:, 16 + h:17 + h]), [("tt", h), "pd2", "pd4"], [("LL", ls)],
                    _actc(512)).tset = "B"
                add("dve", lambda e, h=h, ls=ls: e.tensor_tensor_scan(
                    out=bb[h][:], data0=scanmask, data1=LL[ls][:], initial=0.0,
                    op0=ALU.mult, op1=ALU.add), [("LL", ls), "cb"], [("bb", h)], 1250)
                add("act", lambda e, h=h: e.activation(
                    out=decb[:, h, 8 * j:8 * j + 8], in_=bb[h][:, 63:TB:64], func=AF.Exp),
                    [("bb", h)], [("dec", h, j)], 250).tset = "B"
                add("act", lambda e, h=h, ls=ls: e.activation(out=LL[ls][:], in_=bb[h][:],
                                                              func=AF.Exp, scale=-1.0),
                    [("bb", h)], [("LL", ls)], _actc(512)).tset = "B"
                add("act", lambda e, h=h: e.activation(out=bb[h][:], in_=bb[h][:], func=AF.Exp),
                    [("bb", h)], [("bb", h)], _actc(512)).tset = "B"
                add("dve", lambda e, h=h, ls=ls: e.scalar_tensor_tensor(
                    out=kinvT[:, h, :], in0=tt[h][:], scalar=1.0, in1=LL[ls][:],
                    op0=ALU.subtract, op1=ALU.mult), [("tt", h), ("LL", ls)], [("kinvT", h)],
                    _dvec(512))

        def stage_Q(j):
            for h in range(4):
                vb = inproj_fm(j, GQ, h)
                add("dve", lambda e, vb=vb, h=h: e.scalar_tensor_tensor(
                    out=qdec[:, h, :], in0=PS(vb)[:], scalar=pd[:, 12 + h:13 + h], in1=bb[h][:],
                    op0=ALU.mult, op1=ALU.mult), [vb, ("bb", h), "pd3"], [("qdec", h)],
                    _dvec(512, True))

        def stage_C(j, c):
            s_ = c % 2
            vb = inproj_fm(j, GU, c)
            add("act", lambda e, vb=vb: e.copy(out=usb[0][:], in_=PS(vb)[:]),
                [vb], [("usb", 0)], _actc(512))
            vb = inproj_fm(j, GGC, c)
            add("dve", lambda e, vb=vb: e.tensor_tensor(
                out=cu[c][:, 2:TB + 2], in0=PS(vb)[:], in1=usb[0][:], op=ALU.mult),
                [vb, ("usb", 0)], [("cub", c)], _dvec(512, True))
            add("dve", lambda e: e.tensor_scalar(
                out=ct[s_][:], in0=cu[c][:, 2:TB + 2], scalar1=pp[:, 8 + 8 + c:8 + 8 + c + 1],
                scalar2=None, op0=ALU.mult), [("cub", c), "pp"], [("ct", s_)],
                _dvec(512, fast=True))
            for (lo, col) in ((1, 8 + 4 + c), (0, 8 + c)):
                add("dve", lambda e, lo=lo, col=col: e.scalar_tensor_tensor(
                    out=ct[s_][:], in0=cu[c][:, lo:TB + lo], scalar=pp[:, col:col + 1],
                    in1=ct[s_][:], op0=ALU.mult, op1=ALU.add),
                    [("cub", c), ("cuh", c), ("ct", s_), "pp"], [("ct", s_)], _dvec(512))
            add("dve", lambda e: e.tensor_copy(out=cu[c][:, 0:2], in_=cu[c][:, TB:TB + 2]),
                [("cub", c)], [("cuh", c)], 120)
            vb = inproj_fm(j, GGB, c)
            add("dve", lambda e, vb=vb: e.tensor_tensor(
                out=ct[s_][:], in0=PS(vb)[:], in1=ct[s_][:], op=ALU.mult),
                [vb, ("ct", s_)], [("ct", s_)], _dvec(512, True))
            add("act", lambda e: e.activation(out=osq[s_][:], in_=ct[s_][:], func=AF.Square),
                [("ct", s_)], [("osq", s_)], _actc(512))
            vbz = inproj_fm(j, GZB, c)
            add("act", lambda e: e.activation(out=sa[s_][:], in_=PS(vbz)[:], func=AF.Silu),
                [vbz], [("sa", s_)], _actc(512)).tset = "A"
            gn_sums(osq[s_], ("osq", s_), 2, 64 * EPS, s_, (0, 1))
            vb3 = gn_bc(2, s_)
            add("dve", lambda e: e.scalar_tensor_tensor(
                out=ct[s_][:], in0=ct[s_][:], scalar=pd[:, 24 + c:25 + c], in1=PS(vb3)[:],
                op0=ALU.mult, op1=ALU.mult), [("ct", s_), vb3, "pd6"], [("ct", s_)],
                _dvec(512, True))
            add("dve", lambda e: e.tensor_tensor(
                out=mixT[:, 4 + c, :], in0=ct[s_][:], in1=sa[s_][:], op=ALU.mult),
                [("ct", s_), ("sa", s_)], [("mixT", 4 + c)], _dvec(512))

        def stage_KT(j):
            for i in range(4):
                tv = P.vb("tr")

                def fn(e, i=i, tv=tv):
                    ins = None
                    for h in range(4):
                        ins = e.transpose(TR(tv)[:, h * 128:(h + 1) * 128],
                                          kinvT[:, h, i * 128:(i + 1) * 128], ident)
                    return ins
                add("pe", fn, [("kinvT", h) for h in range(4)] + ["cbi"], [tv], 4 * 100.0)
                add("act", lambda e, i=i, tv=tv: e.copy(out=ktok[:, i, :], in_=TR(tv)[:, 0:512]),
                    [tv], [("ktok", i)], _actc(512))

        def stage_SC(j):
            for i in range(4):
                vb = P.vb()

                def fn(e, i=i, vb=vb):
                    ins = None
                    for h in range(4):
                        ins = e.matmul(PS(vb)[:, h * 128:(h + 1) * 128],
                                       kinvT[:, h, i * 128:(i + 1) * 128],
                                       qdec[:, h, i * 128:(i + 1) * 128], start=True, stop=True)
                    return ins
                add("pe", fn, [("kinvT", h) for h in range(4)] + [("qdec", h) for h in range(4)],
                    [vb], 4 * _mmc(128))
                add("dve", lambda e, i=i, vb=vb: e.tensor_tensor(
                    out=scm[:, i, :].rearrange("p (h t) -> p h t", h=4),
                    in0=PS(vb)[:, :].rearrange("p (h t) -> p h t", h=4),
                    in1=mask1.unsqueeze(1).to_broadcast([128, 4, 128]), op=ALU.mult),
                    [vb, "cb"], [("scm", i)], _dvec(512, True))

        def stage_U(j, hp):
            heads = (2 * hp, 2 * hp + 1)
            for n in range(8):
                gn = 8 * j + n
                i = n // 2
                po = (n % 2) * 64
                vb = P.vb()

                def fn(e, i=i, po=po, vb=vb):
                    ins = None
                    for h in heads:
                        ins = e.matmul(PS(vb)[:, h * 128:(h + 1) * 128],
                                       ktok[po:po + 64, i, h * 128:(h + 1) * 128],
                                       vbf[po:po + 64, i, h * 128:(h + 1) * 128],
                                       start=True, stop=True)
                    return ins
                add("pe", fn, [("ktok", i), ("vbf", i)], [vb], 2 * _mmc(128))
                for h in heads:
                    hs = slice(h * 128, (h + 1) * 128)
                    hl = slice((h % 2) * 128, (h % 2 + 1) * 128)
                    if gn == 0:
                        add("dve", lambda e, vb=vb, hs=hs: e.tensor_copy(out=Tst[:, hs],
                                                                       in_=PS(vb)[:, hs]),
                            [vb], [("T", h)], _dvec(128, True))
                    else:
                        add("dve", lambda e, vb=vb, hs=hs, h=h, gn=gn: e.scalar_tensor_tensor(
                            out=Tst[:, hs], in0=Tst[:, hs], scalar=decb[:, h, gn - 1:gn],
                            in1=PS(vb)[:, hs], op0=ALU.mult, op1=ALU.add),
                            [vb, ("T", h), ("dec", h, (gn - 1) // 8)], [("T", h)],
                            _dvec(128, True) + 100)
                    if n < 7:
                        dst, dkey = Sbf[:, n, hl], ("Sbf", n, h % 2)
                    else:
                        dst, dkey = Sb0[:, (j + 1) % 2, hs], ("Sb0", (j + 1) % 2, h)
                    add("act", lambda e, hs=hs, h=h, gn=gn, dst=dst: e.mul(
                        out=dst, in_=Tst[:, hs], mul=decb[:, h, gn:gn + 1]),
                        [("T", h), ("dec", h, gn // 8)], [dkey], _actc(128) + 100)

        def stage_O(j, h):
            hs = slice(h * 128, (h + 1) * 128)
            s_ = h % 2
            vb = P.vb()

            def fn(e):
                ins = None
                for i in range(4):
                    ins = e.matmul(PS(vb)[:, i * 128:(i + 1) * 128], vbf[:, i, hs],
                                   scm[:, i, hs], start=True, stop=False)
                    for n in (2 * i, 2 * i + 1):
                        gn = 8 * j + n
                        if gn == 0:
                            continue
                        ins = e.matmul(PS(vb)[:, n * 64:(n + 1) * 64],
                                       (Sb0[:, j % 2, hs] if n == 0 else
                                        Sbf[:, n - 1, (h % 2) * 128:(h % 2 + 1) * 128]),
                                       qdec[:, h, n * 64:(n + 1) * 64],
                                       start=False, stop=(n == 2 * i + 1))
                return ins
            add("pe", fn,
                [("vbf", i) for i in range(4)] + [("scm", i) for i in range(4)]
                + [("Sbf", n, h % 2) for n in range(7)] + [("Sb0", j % 2, h)] + [("qdec", h)],
                [vb], 4 * _mmc(128) + 8 * _mmc(64))
            add("act", lambda e: e.activation(out=osq[s_][:], in_=PS(vb)[:], func=AF.Square),
                [vb], [("osq", s_)], _actc(512))
            vbz = inproj_fm(j, GZA, h)
            add("act", lambda e: e.activation(out=sa[s_][:], in_=PS(vbz)[:], func=AF.Silu),
                [vbz], [("sa", s_)], _actc(512)).tset = "A"
            gn_sums(osq[s_], ("osq", s_), 1, 128 * EPS, s_, (s_,))
            vb3 = gn_bc(1, s_)
            add("act", lambda e: e.copy(out=rs[s_][:], in_=PS(vb3)[:]),
                [vb3], [("rs", s_)], _actc(512))
            add("dve", lambda e: e.scalar_tensor_tensor(
                out=rs[s_][:], in0=PS(vb)[:], scalar=pd[:, 20 + h:21 + h], in1=rs[s_][:],
                op0=ALU.mult, op1=ALU.mult), [vb, ("rs", s_), "pd5"], [("rs", s_)],
                _dvec(512, True))
            add("dve", lambda e: e.tensor_tensor(
                out=mixT[:, h, :], in0=rs[s_][:], in1=sa[s_][:], op=ALU.mult),
                [("rs", s_), ("sa", s_)], [("mixT", h)], _dvec(512))

        def stage_OUT(j):
            for i in range(4):
                ti = 4 * j + i
                s_ = alloc_xslot()
                xs = ti % 2
                load_x(ti, s_)
                for hf in range(2):
                    vb = P.vb()
                    pairs = [(mixT[:, k, i * 128:(i + 1) * 128], wout[:, k, hf * 512:(hf + 1) * 512])
                             for k in range(8)]
                    mm_group(vb, pairs, [("mixT", k) for k in range(8)] + [("wout", hf)], 512, tag="out")
                    add("dve", lambda e, vb=vb, s_=s_, hf=hf: e.tensor_tensor(
                        out=xt[s_][:, hf * 512:(hf + 1) * 512], in0=PS(vb)[:],
                        in1=xt[s_][:, hf * 512:(hf + 1) * 512], op=ALU.add),
                        [vb, ("xt", s_)], [("xt", s_)], _dvec(512, True))
                add("act", lambda e, s_=s_, ti=ti, xs=xs: e.activation(
                    out=xn[xs][:], in_=xt[s_][:], func=AF.Square, accum_out=ss2[:, ti:ti + 1]),
                    [("xt", s_), "ss2"], [("ss2", ti), ("xn", xs)], _actc(1024))
                add("pool", lambda e, ti=ti: e.tensor_scalar(
                    out=rstd2[:, ti:ti + 1], in0=ss2[:, ti:ti + 1], scalar1=1.0 / D,
                    scalar2=float(EPS), op0=ALU.mult, op1=ALU.add), [("ss2", ti)],
                    [("rstd2", ti)], 500)
                add("pool", lambda e, ti=ti: e.tensor_tensor(
                    out=rstd2[:, ti:ti + 1], in0=rstd2[:, ti:ti + 1], in1=pd[:, 28:29], op=ALU.pow),
                    [("rstd2", ti), "mh"], [("rstd2", ti)], 700)
                add("dve", lambda e, s_=s_, ti=ti: e.scalar_tensor_tensor(
                    out=xt[s_][:], in0=xt[s_][:], scalar=rstd2[:, ti:ti + 1], in1=fgbc[:],
                    op0=ALU.mult, op1=ALU.mult),
                    [("xt", s_), ("rstd2", ti), "fgbc"], [("xt", s_)], _dvec(1024))
                dma("sp", out_d[ti * 128:(ti + 1) * 128, :], xt[s_][:], [("xt", s_)], [("out", ti)],
                    524288)

        stage_X(0)
        for j in range(NB):
            stage_F(j)
            stage_V(j)
            stage_CH(j)
            stage_Q(j)
            stage_C(j, 0)
            stage_C(j, 1)
            stage_KT(j)
            stage_SC(j)
            stage_U(j, 0)
            stage_C(j, 2)
            if j + 1 < NB:
                stage_X(j + 1)
            stage_C(j, 3)
            stage_O(j, 0)
            stage_O(j, 1)
            stage_U(j, 1)
            stage_O(j, 2)
            stage_O(j, 3)
            stage_OUT(j)

        P.add("sp", lambda e: e.nop(), [("out", ti) for ti in range(NT)], [], cost=50)

        P.schedule({"ps": list(range(NPS)), "tr": list(range(NTR))})
        P.finalize()
        print("sbuf bytes remaining:", nc.sbuf_bytes_remaining,
              {e: len(P.ops[e]) for e in P.ENGS}, "est makespan us: %.1f" % (P.makespan / 1e3))
        with nc.Block() as block:
            P.emit(nc, block, sems, dsems)
    return nc


def _consts():
    cb = np.zeros((128, 1280), np.float32)
    s = np.arange(128)[:, None]
    t = np.arange(128)[None, :]
    m = ((s // 64) == (t // 64)) & (s <= t)
    cb[:, 384:512] = m.astype(np.float32)
    sm = np.ones((512,), np.float32)
    sm[0::64] = 0.0
    cb[:, 512:1024] = sm[None, :]
    cb[:, 0:128] = np.eye(128, dtype=np.float32)
    cb[:, 128:256] = 1.0
    cb[:, 256] = (np.arange(128) < 64)
    cb[:, 257] = (np.arange(128) >= 64)
    cb[:, 1024 + 64:1024 + 192] = 1.0
    return cb


_NC_CACHE = {}


def kernel(x, norm_gain, w_in, lb_logits, conv_w, hgrn_norm_gain, conv_norm_gain, w_out,
           final_norm_gain):
    x = np.asarray(x, np.float32)
    B = x.shape[0]
    w_in2 = np.ascontiguousarray(np.asarray(w_in, np.float32)[0])
    w_out2 = np.ascontiguousarray(np.asarray(w_out, np.float32)[0])
    lb = np.asarray(lb_logits, np.float32)
    cw = np.asarray(conv_w, np.float32)[0]
    hg = np.asarray(hgrn_norm_gain, np.float32)[0]
    cg = np.asarray(conv_norm_gain, np.float32)[0]
    pp = np.zeros((128, 32), np.float32)
    pp[:, 0:4] = lb[0].reshape(4, 128).T
    pp[:, 4:8] = lb[1].reshape(4, 128).T
    for jj in range(3):
        pp[:, 8 + 4 * jj: 12 + 4 * jj] = cw[jj].reshape(4, 128).T
    pp[:, 20:24] = hg.reshape(4, 128).T
    pp[:, 24:28] = cg.reshape(4, 128).T
    cb = _consts()
    ng = np.ascontiguousarray(np.asarray(norm_gain, np.float32).reshape(1, D))
    fg = np.ascontiguousarray(np.asarray(final_norm_gain, np.float32).reshape(1, D))
    if "nc" not in _NC_CACHE:
        _NC_CACHE["nc"] = build_nc()
    nc = _NC_CACHE["nc"]
    in_maps = [{"x": np.ascontiguousarray(x[b]), "w_in": w_in2, "w_out": w_out2, "pp": pp,
                "cb": cb, "ng": ng, "fg": fg} for b in range(B)]
    res = run_bass_kernel_spmd(nc, in_maps, core_ids=list(range(B)))
    return np.stack([np.asarray(r["out"]) for r in res.results], axis=0).astype(np.float32)
```
